# Optimizing a Trainium2 kernel written in Bass

```python
import math
import jax, jax.numpy as jnp
from jax import lax
import numpy as np

D_MODEL = 2048
BATCH = 2
SEQ = 16384
DEPTH = 1

D_MIX = D_MODEL
C_CONV = D_MIX // 2
CONV_GROUP_DIM = 128
N_CONV_GROUPS = C_CONV // CONV_GROUP_DIM
CONV_WIDTH = 31
D_DN = D_MIX - C_CONV
DN_HEAD_DIM = 128
N_DN_HEADS = D_DN // DN_HEAD_DIM
SHORT_CONV_WIDTH = 4
CHUNK = 64
D_FF = 4 * D_MODEL
EPS = 1e-6

OFF_CONF = 0
OFF_QKV = OFF_CONF + 2 * C_CONV
OFF_Z = OFF_QKV + 3 * D_DN
OFF_B = OFF_Z + D_DN
OFF_A = OFF_B + N_DN_HEADS
D_IN = OFF_A + N_DN_HEADS

kernel_name = "hybrid_conformer_gated_deltanet_block"


def rms_norm(x, w):
    xf = x.astype(jnp.float32)
    y = xf * lax.rsqrt(jnp.mean(xf * xf, axis=-1, keepdims=True) + EPS)
    return (y * w.astype(jnp.float32)).astype(x.dtype)


def causal_depthwise_conv(x, w):
    k_width, ch = w.shape
    return lax.conv_general_dilated(
        x, w[:, None, :].astype(x.dtype), window_strides=(1,),
        padding=[(k_width - 1, 0)], dimension_numbers=("NWC", "WIO", "NWC"),
        feature_group_count=ch)


def l2_normalize(x):
    return x * lax.rsqrt(jnp.sum(x * x, axis=-1, keepdims=True) + EPS)


def conformer_conv_group(p, b_glu, dw_w, dw_b, ln_g, ln_b):
    bsz, seq, _ = p.shape
    p = p + b_glu.astype(p.dtype)
    h = p[..., :C_CONV] * jax.nn.sigmoid(p[..., C_CONV:])
    h = causal_depthwise_conv(h, dw_w) + dw_b.astype(h.dtype)
    hg = h.astype(jnp.float32).reshape(bsz, seq, N_CONV_GROUPS, CONV_GROUP_DIM)
    mu = jnp.mean(hg, axis=-1, keepdims=True)
    var = jnp.mean(jnp.square(hg - mu), axis=-1, keepdims=True)
    hn = ((hg - mu) * lax.rsqrt(var + EPS)).reshape(bsz, seq, C_CONV)
    hn = hn * ln_g.astype(jnp.float32) + ln_b.astype(jnp.float32)
    return jax.nn.silu(hn).astype(p.dtype)


def chunked_gated_delta_rule(q, k, v, g, beta):
    bsz, seq, nh, dk = q.shape
    dv = v.shape[-1]
    nc = seq // CHUNK
    q = q * (dk ** -0.5)

    def to_chunks(t):
        return t.reshape(bsz, nc, CHUNK, nh, -1).transpose(0, 3, 1, 2, 4)

    q, k, v = to_chunks(q), to_chunks(k), to_chunks(v)
    g = g.reshape(bsz, nc, CHUNK, nh).transpose(0, 3, 1, 2)
    beta = beta.reshape(bsz, nc, CHUNK, nh).transpose(0, 3, 1, 2)
    g = jnp.cumsum(g, axis=-1)

    causal = jnp.tril(jnp.ones((CHUNK, CHUNK), dtype=bool))
    strict = jnp.tril(jnp.ones((CHUNK, CHUNK), dtype=bool), -1)
    decay = jnp.exp(jnp.where(causal, g[..., :, None] - g[..., None, :], -jnp.inf))

    k_beta = k * beta[..., None]
    v_beta = v * beta[..., None]
    a = jnp.where(strict, jnp.einsum("bhncd,bhnsd->bhncs", k_beta, k) * decay, 0.0)
    eye = jnp.eye(CHUNK, dtype=jnp.float32)
    t_mat = lax.linalg.triangular_solve(eye + a, jnp.broadcast_to(eye, a.shape),
                                        left_side=True, lower=True, unit_diagonal=True)
    u = jnp.einsum("bhncs,bhnsd->bhncd", t_mat, v_beta)
    w = jnp.einsum("bhncs,bhnsd->bhncd", t_mat, k_beta * jnp.exp(g)[..., None])
    q_g = q * jnp.exp(g)[..., None]
    k_g = k * jnp.exp(g[..., -1:] - g)[..., None]
    last_decay = jnp.exp(g[..., -1])
    intra = jnp.einsum("bhncd,bhnsd->bhncs", q, k) * decay

    def step(state, inp):
        w_i, u_i, qg_i, kg_i, ld_i = inp
        v_new = u_i - jnp.einsum("bhcd,bhde->bhce", w_i, state)
        o_inter = jnp.einsum("bhcd,bhde->bhce", qg_i, state)
        state = state * ld_i[..., None, None] + jnp.einsum("bhcd,bhce->bhde", kg_i, v_new)
        return state, (v_new, o_inter)

    xs = tuple(jnp.moveaxis(t, 2, 0) for t in (w, u, q_g, k_g, last_decay))
    s0 = jnp.zeros((bsz, nh, dk, dv), jnp.float32)
    _, (v_new, o_inter) = lax.scan(step, s0, xs)
    v_new = jnp.moveaxis(v_new, 0, 2)
    o_inter = jnp.moveaxis(o_inter, 0, 2)
    o = o_inter + jnp.einsum("bhncs,bhnse->bhnce", intra, v_new)
    return o.transpose(0, 2, 3, 1, 4).reshape(bsz, seq, nh, dv)


def gated_deltanet_group(p, conv_w, a_log, dt_bias, norm_w):
    bsz, seq, _ = p.shape
    dtype = p.dtype
    qkv = jax.nn.silu(causal_depthwise_conv(p[..., :3 * D_DN], conv_w))
    qkv = qkv.astype(jnp.float32).reshape(bsz, seq, 3, N_DN_HEADS, DN_HEAD_DIM)
    q = l2_normalize(qkv[:, :, 0])
    k = l2_normalize(qkv[:, :, 1])
    v = qkv[:, :, 2]
    z = p[..., 3 * D_DN:4 * D_DN].astype(jnp.float32).reshape(bsz, seq, N_DN_HEADS, DN_HEAD_DIM)
    b_raw = p[..., 4 * D_DN:4 * D_DN + N_DN_HEADS].astype(jnp.float32)
    a_raw = p[..., 4 * D_DN + N_DN_HEADS:4 * D_DN + 2 * N_DN_HEADS].astype(jnp.float32)
    beta = jax.nn.sigmoid(b_raw)
    g = -jnp.exp(a_log.astype(jnp.float32)) * jax.nn.softplus(a_raw + dt_bias.astype(jnp.float32))
    o = chunked_gated_delta_rule(q, k, v, g, beta)
    o = o * lax.rsqrt(jnp.mean(o * o, axis=-1, keepdims=True) + EPS) * norm_w.astype(jnp.float32)
    o = o * jax.nn.silu(z)
    return o.reshape(bsz, seq, D_DN).astype(dtype)


def setup_inputs(seed: int = 0) -> dict:
    key = jax.random.key(seed)
    ks = jax.random.split(key, 20)
    f32 = jnp.float32
    x = jax.random.normal(ks[0], (BATCH, SEQ, D_MODEL), f32)
    norm1_w = 1.0 + 0.02 * jax.random.normal(ks[1], (DEPTH, D_MODEL), f32)
    w_in = jax.random.normal(ks[2], (DEPTH, D_MODEL, D_IN), f32) * D_MODEL ** -0.5
    b_glu = 0.02 * jax.random.normal(ks[3], (DEPTH, 2 * C_CONV), f32)
    conf_dw_w = jax.random.normal(ks[4], (DEPTH, CONV_WIDTH, C_CONV), f32) * CONV_WIDTH ** -0.5
    conf_dw_b = 0.02 * jax.random.normal(ks[5], (DEPTH, C_CONV), f32)
    conf_ln_g = 1.0 + 0.02 * jax.random.normal(ks[6], (DEPTH, C_CONV), f32)
    conf_ln_b = 0.02 * jax.random.normal(ks[7], (DEPTH, C_CONV), f32)
    dn_conv_w = jax.random.normal(ks[8], (DEPTH, SHORT_CONV_WIDTH, 3 * D_DN), f32) * SHORT_CONV_WIDTH ** -0.5
    dn_a_log = jnp.log(jax.random.uniform(ks[9], (DEPTH, N_DN_HEADS), f32, 1.0, 16.0))
    dt = jnp.exp(jax.random.uniform(ks[10], (DEPTH, N_DN_HEADS), f32, math.log(1e-3), math.log(1e-1)))
    dn_dt_bias = dt + jnp.log(-jnp.expm1(-dt))
    dn_norm_w = 1.0 + 0.02 * jax.random.normal(ks[11], (DEPTH, DN_HEAD_DIM), f32)
    w_out = jax.random.normal(ks[12], (DEPTH, D_MIX, D_MODEL), f32) * D_MIX ** -0.5
    norm2_w = 1.0 + 0.02 * jax.random.normal(ks[13], (DEPTH, D_MODEL), f32)
    w_mlp_up = jax.random.normal(ks[14], (DEPTH, D_MODEL, D_FF), f32) * D_MODEL ** -0.5
    w_mlp_down = jax.random.normal(ks[15], (DEPTH, D_FF, D_MODEL), f32) * D_FF ** -0.5
    final_norm_w = 1.0 + 0.02 * jax.random.normal(ks[16], (D_MODEL,), f32)
    return {"x": x, "norm1_w": norm1_w, "w_in": w_in, "b_glu": b_glu,
            "conf_dw_w": conf_dw_w, "conf_dw_b": conf_dw_b, "conf_ln_g": conf_ln_g,
            "conf_ln_b": conf_ln_b, "dn_conv_w": dn_conv_w, "dn_a_log": dn_a_log,
            "dn_dt_bias": dn_dt_bias, "dn_norm_w": dn_norm_w, "w_out": w_out,
            "norm2_w": norm2_w, "w_mlp_up": w_mlp_up, "w_mlp_down": w_mlp_down,
            "final_norm_w": final_norm_w}


def reference(x, norm1_w, w_in, b_glu, conf_dw_w, conf_dw_b, conf_ln_g, conf_ln_b,
              dn_conv_w, dn_a_log, dn_dt_bias, dn_norm_w, w_out, norm2_w,
              w_mlp_up, w_mlp_down, final_norm_w):
    h = x
    for l in range(DEPTH):
        u = rms_norm(h, norm1_w[l])
        proj = u @ w_in[l]
        conv_out = conformer_conv_group(proj[..., OFF_CONF:OFF_QKV], b_glu[l], conf_dw_w[l],
                                        conf_dw_b[l], conf_ln_g[l], conf_ln_b[l])
        dn_out = gated_deltanet_group(proj[..., OFF_QKV:], dn_conv_w[l], dn_a_log[l],
                                      dn_dt_bias[l], dn_norm_w[l])
        mix = jnp.concatenate([conv_out, dn_out], axis=-1)
        h = h + mix @ w_out[l]
        m = rms_norm(h, norm2_w[l]) @ w_mlp_up[l]
        h = h + jnp.square(jax.nn.relu(m)) @ w_mlp_down[l]
    return rms_norm(h, final_norm_w)
```

```python
import numpy as np
from contextlib import ExitStack
import concourse.bass as bass
import concourse.mybir as mybir
from concourse.bass_utils import run_bass_kernel_spmd
import ml_dtypes

F32 = mybir.dt.float32
BF16 = mybir.dt.bfloat16
AF = mybir.ActivationFunctionType
ALU = mybir.AluOpType
AX = mybir.AxisListType

D = 2048
DFF = 8192
EPS = 1e-6
NEG = -30000.0


class Prog:
    ENG = ("tensor", "vector", "scalar", "gpsimd", "sync")

    def __init__(self, nc, es):
        self.nc, self.es = nc, es
        self.q = {e: [] for e in self.ENG}
        self.cnt = {e: 0 for e in self.ENG}
        self.sem = {e: es.enter_context(nc.semaphore("s_" + e)) for e in self.ENG}
        self.waited = {e: {} for e in self.ENG}
        self.bufs = {}
        self.dsem = {}

    def _deps(self, eng, reads, writes):
        need = {}

        def add(name, sem, val):
            if eng == "tensor" and name == "tensor":
                return
            if need.get(name, (None, 0))[1] < val:
                need[name] = (sem, val)

        for k in reads:
            st = self.bufs.get(k)
            if st and st[0]:
                add(*st[0])
        for k in writes:
            st = self.bufs.get(k)
            if st:
                if st[0]:
                    add(*st[0])
                for name, (sem, val) in st[1].items():
                    add(name, sem, val)
        waits = []
        for name, (sem, val) in need.items():
            if self.waited[eng].get(name, 0) < val:
                self.waited[eng][name] = val
                waits.append((sem, val))
        return waits

    def _commit(self, ev, reads, writes):
        name, sem, val = ev
        for k in reads:
            st = self.bufs.setdefault(k, [None, {}])
            st[1][name] = (sem, val)
        for k in writes:
            self.bufs[k] = [ev, {}]

    def op(self, eng, fn, reads=(), writes=()):
        waits = self._deps(eng, reads, writes)
        self.cnt[eng] += 1
        ev = (eng, self.sem[eng], self.cnt[eng])
        self.q[eng].append((waits, fn, self.sem[eng], 1))
        self._commit(ev, reads, writes)

    def dma(self, eng, fn, key, reads=(), writes=()):
        waits = self._deps(eng, reads, writes)
        if key not in self.dsem:
            self.dsem[key] = [self.es.enter_context(self.nc.semaphore("d_%d" % len(self.dsem))), 0]
        d = self.dsem[key]
        d[1] += 16
        ev = ("dma_%s" % (key,), d[0], d[1])
        self.q[eng].append((waits, fn, d[0], 16))
        self._commit(ev, reads, writes)

    def emit(self):
        with self.nc.Block() as block:
            for e in self.ENG:
                items = self.q[e]
                finals = [(d[0], d[1]) for d in self.dsem.values()] if e == "sync" else []

                def body(engine, items=items, finals=finals):
                    for waits, fn, sem, inc in items:
                        for s, v in waits:
                            engine.wait_ge(s, v)
                        fn(engine).then_inc(sem, inc)
                    for s, v in finals:
                        engine.wait_ge(s, v)

                if items or finals:
                    getattr(block, e)(body)


def _consts():
    c = np.zeros((128, 8, 128), np.float32)
    i = np.arange(128)
    same = (i[:, None] // 64) == (i[None, :] // 64)
    c[:, 0] = np.eye(128)
    c[:, 1] = 1.0
    c[:, 2] = ((i[:, None] <= i[None, :]) & same)
    c[:, 3] = ((i[:, None] > i[None, :]) & same)
    c[:, 4] = (i[:, None] < 64) * np.ones((1, 128))
    c[:, 5] = (i[:, None] >= 64) * np.ones((1, 128))
    c[:, 6] = np.where((i[None, :] <= i[:, None]) & same, 0.0, NEG)
    c[:, 7] = ((i[None, :] < i[:, None]) & same)
    return c.reshape(128, 1024)


PV_N1 = 0
PV_BV = 16
PV_BG = 18
PV_DWB = 20
PV_LNG = 22
PV_LNB = 24
PV_DWW = 26
PV_DNW = 88
PV_ALOG = 112
PV_DTB = 114
PV_NW = 116
PV_N = 244

NWC = 1540


def build_phase_a(seq):
    T = 512
    ntiles = seq // T
    nc = bass.Bass("TRN2", target_bir_lowering=False)
    x = nc.dram_tensor("x", [seq, D], F32, kind="ExternalInput").ap()
    win = nc.dram_tensor("win", [D, NWC], F32, kind="ExternalInput").ap()
    pvec_d = nc.dram_tensor("pvec", [128, PV_N], F32, kind="ExternalInput").ap()
    consts_d = nc.dram_tensor("consts", [128, 1024], F32, kind="ExternalInput").ap()
    mixT = nc.dram_tensor("mixT", [512, seq], BF16, kind="ExternalOutput").ap()

    es = ExitStack()
    with es:
        P = Prog(nc, es)

        def sb(name, shape, dt):
            return es.enter_context(nc.sbuf_tensor("sb_" + name, shape, dt))

        def ps(name, shape, dt):
            return es.enter_context(nc.psum_tensor("ps_" + name, shape, dt))

        cst = sb("cst", [128, 8, 128], F32)
        pv = sb("pv", [128, PV_N], F32)
        identb = sb("identb", [128, 128], BF16)
        onesb = sb("onesb", [128, 128], BF16)
        wb = sb("wb", [128, 16, NWC], BF16)
        dgc = sb("dgc", [128, 2, 31, 128], BF16)
        dgs = sb("dgs", [128, 2, 3, 4, 128], BF16)
        nega = sb("nega", [128, 2], F32)
        nones = sb("nones", [128, 128], F32)
        xt = [sb("xt%d" % i, [128, D], F32) for i in range(3)]
        ub = [sb("ub%d" % i, [128, D], BF16) for i in range(2)]
        junk = sb("junk", [128, D], BF16)
        st4 = sb("st4", [128, 8], F32)
        uT = sb("uT", [128, 16, T], BF16)
        hbuf = sb("hbuf", [128, 2, 30 + T], BF16)
        xc = sb("xc", [128, 2, 3, 3 + T], BF16)
        f32t = [sb("f32t%d" % i, [128, T], F32) for i in range(6)]
        b16t = [sb("b16t%d" % i, [128, T], BF16) for i in range(4)]
        qn = sb("qn", [128, T], BF16)
        kn = sb("kn", [128, T], BF16)
        qgT = sb("qgT", [128, T], BF16)
        gate = sb("gate", [128, 4, 256], F32)
        sc = sb("sc", [128, 16, 4, 2], F32)
        GL = sb("GL", [128, 4, 128], F32)
        DD = sb("DD", [128, 4, 128], F32)
        DS = sb("DS", [128, 4, 128], F32)
        Xn = [sb("Xn%d" % i, [128, 4, 128], BF16) for i in range(2)]
        Yn = [sb("Yn%d" % i, [128, 4, 128], BF16) for i in range(2)]
        Rn = [sb("Rn%d" % i, [128, 4, 128], BF16) for i in range(2)]
        intra = sb("intra", [128, 4, 128], BF16)
        intraT = sb("intraT", [128, 4, 128], BF16)
        Kbg = sb("Kbg", [128, 4, 128], BF16)
        Kg0 = sb("Kg0", [128, 4, 128], BF16)
        Kg1 = sb("Kg1", [128, 4, 128], BF16)
        Vb = sb("Vb", [128, 4, 128], BF16)
        nWT = sb("nWT", [128, 4, 128], BF16)
        vnew = sb("vnew", [128, 128], BF16)
        Sf = [sb("Sf%d" % h, [128, 128], F32) for h in range(2)]
        Sb = [sb("Sb%d" % h, [128, 128], BF16) for h in range(2)]
        osb = sb("osb", [128, 4, 128], F32)
        omix = sb("omix", [128, 4, 128], BF16)
        outb = [sb("outb%d" % i, [128, T], BF16) for i in range(2)]

        pf = [ps("pf%d" % i, [128, T], F32) for i in range(3)]
        pf_o = ps("pf_o", [128, T], F32)
        pchd = ps("pchd", [128, T], F32)
        pchain = ps("pchain", [128, T], F32)
        pt = [ps("pt%d" % i, [128, 1024], BF16) for i in range(2)]
        rot = {"f": 0, "t": 0, "f32t": 0, "b16t": 0, "outb": 0}

        def nf():
            rot["f"] = (rot["f"] + 1) % 3
            return pf[rot["f"]], ("pf", rot["f"])

        def nt():
            rot["t"] = (rot["t"] + 1) % 2
            return pt[rot["t"]], ("pt", rot["t"])

        def tf():
            rot["f32t"] = (rot["f32t"] + 1) % 6
            return f32t[rot["f32t"]], ("f32t", rot["f32t"])

        def tb():
            rot["b16t"] = (rot["b16t"] + 1) % 4
            return b16t[rot["b16t"]], ("b16t", rot["b16t"])

        evac_rr = [0]

        def evac(out_ap, in_ap, reads, writes):
            evac_rr[0] ^= 1
            if evac_rr[0]:
                P.op("scalar", lambda e: e.copy(out_ap, in_ap), reads, writes)
            else:
                P.op("vector", lambda e: e.tensor_copy(out_ap, in_ap), reads, writes)

        P.dma("sync", lambda e: e.dma_start(out=cst[:].rearrange("p a b -> p (a b)"), in_=consts_d[:, :]), "cst", (), ("cst",))
        P.dma("sync", lambda e: e.dma_start(out=pv[:], in_=pvec_d[:, :]), "pv", (), ("pv",))
        for c in range(16):
            P.dma("gpsimd", lambda e, c=c: e.dma_start(out=wb[:, c, :], in_=win[c * 128:(c + 1) * 128, :]),
                  ("wb", c), (), (("wb", c),))
        WB = tuple(("wb", c) for c in range(16))
        P.op("vector", lambda e: e.tensor_copy(identb[:], cst[:, 0, :]), ("cst",), ("identb",))
        P.op("vector", lambda e: e.tensor_copy(onesb[:], cst[:, 1, :]), ("cst",), ("onesb",))
        for g in range(2):
            for k in range(31):
                P.op("gpsimd", lambda e, g=g, k=k: e.tensor_scalar(dgc[:, g, k, :], cst[:, 0, :], pv[:, PV_DWW + g * 31 + k:PV_DWW + g * 31 + k + 1], None, ALU.mult),
                     ("cst", "pv"), ("dgc",))
        for h in range(2):
            for j in range(3):
                for k in range(4):
                    col = PV_DNW + (h * 3 + j) * 4 + k
                    P.op("gpsimd", lambda e, h=h, j=j, k=k, col=col: e.tensor_scalar(dgs[:, h, j, k, :], cst[:, 0, :], pv[:, col:col + 1], None, ALU.mult),
                         ("cst", "pv"), ("dgs",))
        P.op("scalar", lambda e: e.activation(nega[:], pv[:, PV_ALOG:PV_ALOG + 2], AF.Exp), ("pv",), ("nega",))
        P.op("vector", lambda e: e.tensor_scalar(nega[:], nega[:], -1.0, None, ALU.mult), ("nega",), ("nega",))
        P.op("vector", lambda e: e.memset(hbuf[:], 0.0), (), ("hbuf",))
        P.op("vector", lambda e: e.memset(xc[:], 0.0), (), ("xc",))
        P.op("vector", lambda e: e.memset(vnew[:], 0.0), (), (("vnew",),))
        P.op("gpsimd", lambda e: e.memset(nones[:], -1.0), (), ("nones",))
        for h in range(2):
            P.op("vector", lambda e, h=h: e.memset(Sf[h][:], 0.0), (), (("Sf", h),))
            P.op("vector", lambda e, h=h: e.memset(Sb[h][:], 0.0), (), (("Sb", h),))

        def rstd_small(dst_ap, src_ap, scale, key_r, key_w):
            P.op("vector", lambda e: e.tensor_scalar(dst_ap, src_ap, scale, EPS, ALU.mult, ALU.add), key_r, key_w)
            P.op("scalar", lambda e: e.activation(dst_ap, dst_ap, AF.Ln), key_w, key_w)
            P.op("scalar", lambda e: e.activation(dst_ap, dst_ap, AF.Exp, scale=-0.5), key_w, key_w)

        xslot = [0]
        for ti in range(ntiles):
            t0 = ti * T
            for s in range(4):
                xi = xslot[0] % 3
                ui = xslot[0] % 2
                xslot[0] += 1
                xk, uk = ("xt", xi), ("ub", ui)
                P.dma("sync", lambda e, xi=xi, r0=t0 + s * 128: e.dma_start(out=xt[xi][:], in_=x[r0:r0 + 128, :]), xk, (), (xk,))
                P.op("scalar", lambda e, xi=xi, s=s: e.activation(junk[:], xt[xi][:], AF.Square, accum_out=st4[:, s:s + 1]), (xk,), ("junk", ("st4", s)))
                rstd_small(st4[:, s:s + 1], st4[:, s:s + 1], 1.0 / D, (("st4", s),), (("st4", s),))
                P.op("vector", lambda e, xi=xi, ui=ui, s=s: e.tensor_scalar(ub[ui][:], xt[xi][:], st4[:, s:s + 1], None, ALU.mult), (xk, ("st4", s)), (uk,))
                for c4 in range(4):
                    ptile, pk = nt()
                    for cc in range(4):
                        c = c4 * 4 + cc
                        P.op("tensor", lambda e, ptile=ptile, cc=cc, c=c, ui=ui: e.transpose(ptile[:, cc * 128:(cc + 1) * 128], ub[ui][:, c * 128:(c + 1) * 128], identb[:]),
                             (uk, "identb"), (pk,))
                    P.op("vector", lambda e, ptile=ptile, c4=c4, s=s: e.tensor_tensor(
                        uT[:, c4 * 4:c4 * 4 + 4, s * 128:(s + 1) * 128],
                        ptile[:, 0:512].rearrange("p (a b) -> p a b", a=4),
                        pv[:, PV_N1 + c4 * 4:PV_N1 + c4 * 4 + 4].unsqueeze(2).broadcast_to([128, 4, 128]), ALU.mult),
                        (pk, "pv"), (("uT", s),))
            UT = tuple(("uT", s) for s in range(4))

            def proj(col0, ncols=128):
                pt_, pk_ = nf()
                for c in range(16):
                    P.op("tensor", lambda e, c=c, pt_=pt_: e.matmul(pt_[0:ncols, :], wb[:, c, col0:col0 + ncols], uT[:, c, :], start=(c == 0), stop=(c == 15)),
                         UT + (("wb", c),), (pk_,))
                return pt_, pk_

            for g in range(2):
                pvv, pvk = proj(g * 128)
                pg, pgk = proj(256 + g * 128)
                sg, sgk = tf()
                P.op("scalar", lambda e, sg=sg, pg=pg, g=g: e.activation(sg[:], pg[:], AF.Sigmoid, bias=pv[:, PV_BG + g:PV_BG + g + 1]), (pgk, "pv"), (sgk,))
                P.op("vector", lambda e, sg=sg, pvv=pvv, g=g: e.scalar_tensor_tensor(hbuf[:, g, 30:30 + T], pvv[:], pv[:, PV_BV + g:PV_BV + g + 1], sg[:], ALU.add, ALU.mult),
                     (pvk, sgk, "pv"), (("hbuf", g),))
                pc_, pck = nf()
                for k in range(31):
                    P.op("tensor", lambda e, pc_=pc_, g=g, k=k: e.matmul(pc_[:], dgc[:, g, k, :], hbuf[:, g, k:k + T], start=(k == 0), stop=(k == 30)),
                         ("dgc", ("hbuf", g)), (pck,))
                y, yk = tf()
                P.op("scalar", lambda e, y=y, pc_=pc_, g=g: e.activation(y[:], pc_[:], AF.Identity, bias=pv[:, PV_DWB + g:PV_DWB + g + 1]), (pck, "pv"), (yk,))
                P.op("gpsimd", lambda e, g=g: e.tensor_copy(hbuf[:, g, 0:30], hbuf[:, g, T:T + 30]), (("hbuf", g),), (("hbuf", g),))
                pm, pmk = nf()
                P.op("tensor", lambda e, pm=pm, y=y: e.matmul(pm[:], cst[:, 1, :], y[:], start=True, stop=True), ("cst", yk), (pmk,))
                yc, yck = tf()
                P.op("vector", lambda e, yc=yc, pm=pm, y=y: e.scalar_tensor_tensor(yc[:], pm[:], -1.0 / 128, y[:], ALU.mult, ALU.add), (pmk, yk), (yck,))
                sq, sqk = tf()
                P.op("gpsimd", lambda e, sq=sq, yc=yc: e.tensor_tensor(sq[:], yc[:], yc[:], ALU.mult), (yck,), (sqk,))
                pvr, pvrk = nf()
                P.op("tensor", lambda e, pvr=pvr, sq=sq: e.matmul(pvr[:], cst[:, 1, :], sq[:], start=True, stop=True), ("cst", sqk), (pvrk,))
                rs, rsk = tf()
                P.op("vector", lambda e, rs=rs, pvr=pvr: e.tensor_scalar(rs[:], pvr[:], 1.0 / 128, EPS, ALU.mult, ALU.add), (pvrk,), (rsk,))
                P.op("scalar", lambda e, rs=rs: e.activation(rs[:], rs[:], AF.Ln), (rsk,), (rsk,))
                P.op("scalar", lambda e, rs=rs: e.activation(rs[:], rs[:], AF.Exp, scale=-0.5), (rsk,), (rsk,))
                P.op("vector", lambda e, yc=yc, rs=rs: e.tensor_tensor(yc[:], yc[:], rs[:], ALU.mult), (yck, rsk), (yck,))
                P.op("gpsimd", lambda e, yc=yc, g=g: e.tensor_scalar(yc[:], yc[:], pv[:, PV_LNG + g:PV_LNG + g + 1], pv[:, PV_LNB + g:PV_LNB + g + 1], ALU.mult, ALU.add),
                     (yck, "pv"), (yck,))
                oi = rot["outb"] = (rot["outb"] + 1) % 2
                ok = ("outb", oi)
                P.op("scalar", lambda e, yc=yc, oi=oi: e.activation(outb[oi][:], yc[:], AF.Silu), (yck,), (ok,))
                P.dma("sync", lambda e, oi=oi, g=g, t0=t0: e.dma_start(out=mixT[g * 128:(g + 1) * 128, t0:t0 + T], in_=outb[oi][:]), ok, (ok,), ())

            for s in range(4):
                pz, pzk = nf()
                for c in range(16):
                    P.op("tensor", lambda e, c=c, pz=pz, s=s: e.matmul(pz[:, 0:260], uT[:, c, s * 128:(s + 1) * 128], wb[:, c, 1280:1540], start=(c == 0), stop=(c == 15)),
                         (("uT", s), ("wb", c)), (pzk,))
                P.op("scalar", lambda e, pz=pz, s=s: e.activation(gate[:, s, :], pz[:, 0:256], AF.Silu), (pzk,), (("gate", s),))
                P.op("scalar", lambda e, pz=pz, s=s: e.activation(sc[:, 0, s, :], pz[:, 256:258], AF.Sigmoid), (pzk,), (("sc0", s),))
                P.op("vector", lambda e, pz=pz, s=s: e.tensor_tensor(sc[:, 1, s, :], pz[:, 258:260], pv[:, PV_DTB:PV_DTB + 2], ALU.add), (pzk, "pv"), (("sc1", s),))
            SC1 = tuple(("sc1", s) for s in range(4))
            SC0 = tuple(("sc0", s) for s in range(4))
            P.op("scalar", lambda e: e.activation(sc[:, 1], sc[:, 1], AF.Exp), SC1, ("scg",))
            P.op("scalar", lambda e: e.activation(sc[:, 1], sc[:, 1], AF.Ln, bias=1.0), ("scg",), ("scg",))
            P.op("vector", lambda e: e.tensor_tensor(sc[:, 1], sc[:, 1], nega[:].unsqueeze(1).broadcast_to([128, 4, 2]), ALU.mult), ("scg", "nega"), ("scg",))
            pcs, pcsk = nf()
            for j, m in enumerate((2, 3, 4, 5)):
                P.op("tensor", lambda e, j=j, m=m, pcs=pcs: e.matmul(pcs[:, j * 8:(j + 1) * 8], cst[:, m, :], sc[:, 1].rearrange("p a b -> p (a b)"), start=True, stop=True),
                     ("cst", "scg"), (pcsk,))
            P.op("scalar", lambda e, pcs=pcs: e.activation(sc[:, 2:6].rearrange("p k a b -> p (k a b)"), pcs[:, 0:32], AF.Exp), (pcsk,), ("sce",))
            P.op("vector", lambda e: e.tensor_tensor(sc[:, 6], sc[:, 0], sc[:, 2], ALU.mult), SC0 + ("sce",), ("sc6",))
            P.op("vector", lambda e: e.tensor_scalar(sc[:, 7], sc[:, 0], -1.0, None, ALU.mult), SC0, ("sc7",))
            P.op("vector", lambda e: e.tensor_scalar(sc[:, 8], sc[:, 3], cst[:, 4, 0:1], None, ALU.mult), ("sce", "cst"), ("sc89",))
            P.op("vector", lambda e: e.tensor_scalar(sc[:, 9], sc[:, 3], cst[:, 5, 0:1], None, ALU.mult), ("sce", "cst", "sc89"), ("sc89",))

            for h in range(2):
                act = []
                for j in range(3):
                    pp, ppk = proj(512 + j * 256 + h * 128)
                    xk_ = ("xc", h, j)
                    P.op("scalar", lambda e, pp=pp, h=h, j=j: e.copy(xc[:, h, j, 3:3 + T], pp[:]), (ppk,), (xk_,))
                    pcv, pcvk = nf()
                    for k in range(4):
                        P.op("tensor", lambda e, pcv=pcv, h=h, j=j, k=k: e.matmul(pcv[:], dgs[:, h, j, k, :], xc[:, h, j, k:k + T], start=(k == 0), stop=(k == 3)),
                             ("dgs", xk_), (pcvk,))
                    P.op("gpsimd", lambda e, h=h, j=j: e.tensor_copy(xc[:, h, j, 0:3], xc[:, h, j, T:T + 3]), (xk_,), (xk_,))
                    a_, ak = tf()
                    P.op("scalar", lambda e, a_=a_, pcv=pcv: e.activation(a_[:], pcv[:], AF.Silu), (pcvk,), (ak,))
                    act.append((a_, ak))
                for (a_, ak), dst, dk in ((act[0], qn, "qn"), (act[1], kn, "kn")):
                    sq, sqk = tb()
                    P.op("gpsimd", lambda e, sq=sq, a_=a_: e.tensor_tensor(sq[:], a_[:], a_[:], ALU.mult), (ak,), (sqk,))
                    pss, pssk = nf()
                    P.op("tensor", lambda e, pss=pss, sq=sq: e.matmul(pss[:], onesb[:], sq[:], start=True, stop=True), ("onesb", sqk), (pssk,))
                    rs, rsk = tf()
                    P.op("vector", lambda e, rs=rs, pss=pss: e.tensor_scalar(rs[:], pss[:], 1.0, EPS, ALU.mult, ALU.add), (pssk,), (rsk,))
                    P.op("scalar", lambda e, rs=rs: e.activation(rs[:], rs[:], AF.Ln), (rsk,), (rsk,))
                    P.op("scalar", lambda e, rs=rs: e.activation(rs[:], rs[:], AF.Exp, scale=-0.5), (rsk,), (rsk,))
                    P.op("vector", lambda e, dst=dst, a_=a_, rs=rs: e.tensor_tensor(dst[:], a_[:], rs[:], ALU.mult), (ak, rsk), (dk,))
                vT, vTk = tb()
                P.op("gpsimd", lambda e, vT=vT, a2=act[2][0]: e.tensor_copy(vT[:], a2[:]), (act[2][1],), (vTk,))

                P.op("vector", lambda e, h=h: e.tensor_tensor(GL[:], cst[:, 2, :].unsqueeze(1).broadcast_to([128, 4, 128]),
                                                         sc[:, 1, :, h:h + 1].broadcast_to([128, 4, 128]), ALU.mult), ("cst", "scg"), ("GL",))
                nonesk = "nones"
                pG, pGk = nf()
                for s in range(4):
                    P.op("tensor", lambda e, pG=pG, s=s: e.matmul(pG[:, s * 128:(s + 1) * 128], GL[:, s, :], cst[:, 1, :], start=True, stop=False), ("GL", "cst"), (pGk,))
                    P.op("tensor", lambda e, pG=pG, s=s: e.matmul(pG[:, s * 128:(s + 1) * 128], nones[:], GL[:, s, :], start=False, stop=False), ("GL", nonesk), (pGk,))
                    P.op("tensor", lambda e, pG=pG, s=s: e.matmul(pG[:, s * 128:(s + 1) * 128], cst[:, 0, :], cst[:, 6, :], start=False, stop=True), ("cst",), (pGk,))
                P.op("scalar", lambda e, pG=pG: e.activation(DD[:].rearrange("p a b -> p (a b)"), pG[:], AF.Exp), (pGk,), ("DD",))
                P.op("gpsimd", lambda e: e.tensor_tensor(DS[:], DD[:], cst[:, 7, :].unsqueeze(1).broadcast_to([128, 4, 128]), ALU.mult), ("DD", "cst"), ("DS",))
                pkk, pkkk = nf()
                for s in range(4):
                    P.op("tensor", lambda e, pkk=pkk, s=s: e.matmul(pkk[:, s * 128:(s + 1) * 128], kn[:, s * 128:(s + 1) * 128], kn[:, s * 128:(s + 1) * 128], start=True, stop=True), ("kn",), (pkkk,))
                for s in range(4):
                    P.op("vector", lambda e, pkk=pkk, s=s, h=h: e.scalar_tensor_tensor(Xn[0][:, s, :], pkk[:, s * 128:(s + 1) * 128], sc[:, 7, s, h:h + 1], DS[:, s, :], ALU.mult, ALU.mult),
                         (pkkk, "sc7", "DS"), (("Xn", 0),))
                pqk, pqkk = nf()
                for s in range(4):
                    P.op("tensor", lambda e, pqk=pqk, s=s: e.matmul(pqk[:, s * 128:(s + 1) * 128], qn[:, s * 128:(s + 1) * 128], kn[:, s * 128:(s + 1) * 128], start=True, stop=True), ("qn", "kn"), (pqkk,))
                P.op("vector", lambda e, pqk=pqk: e.scalar_tensor_tensor(intra[:].rearrange("p a b -> p (a b)"), pqk[:], 128.0 ** -0.5, DD[:].rearrange("p a b -> p (a b)"), ALU.mult, ALU.mult),
                     (pqkk, "DD"), ("intra",))

                def transp4(dst, dstk, src_fn, srck, scale_kind=None, h=h):
                    ptile, pk = nt()
                    for s in range(4):
                        P.op("tensor", lambda e, ptile=ptile, s=s, src=src_fn(s): e.transpose(ptile[:, s * 128:(s + 1) * 128], src, identb[:]), (srck, "identb"), (pk,))
                    if scale_kind is None:
                        evac(dst[:].rearrange("p a b -> p (a b)"), ptile[:, 0:512], (pk,), (dstk,))
                    else:
                        P.op("vector", lambda e, ptile=ptile, dst=dst, scale_kind=scale_kind, h=h: e.tensor_tensor(dst[:], ptile[:, 0:512].rearrange("p (a b) -> p a b", a=4),
                                                                            sc[:, scale_kind, :, h:h + 1].broadcast_to([128, 4, 128]), ALU.mult),
                             (pk, "sce", "sc6", "sc89") + SC0, (dstk,))

                transp4(Yn[0], ("Yn", 0), lambda s: Xn[0][:, s, :], ("Xn", 0))
                transp4(intraT, "intraT", lambda s: intra[:, s, :], "intra")
                transp4(Kbg, "Kbg", lambda s: kn[:, s * 128:(s + 1) * 128], "kn", 6)
                transp4(Kg0, "Kg0", lambda s: kn[:, s * 128:(s + 1) * 128], "kn", 8)
                transp4(Kg1, "Kg1", lambda s: kn[:, s * 128:(s + 1) * 128], "kn", 9)
                transp4(Vb, "Vb", lambda s: vT[:, s * 128:(s + 1) * 128], vTk, 0)
                P.op("gpsimd", lambda e: e.tensor_tensor(Rn[0][:], Yn[0][:], cst[:, 0, :].unsqueeze(1).broadcast_to([128, 4, 128]), ALU.add), (("Yn", 0), "cst"), (("Rn", 0),))
                cur = 0
                for p in range(1, 6):
                    nxt = cur ^ 1
                    px, pxk = nf()
                    for s in range(4):
                        P.op("tensor", lambda e, px=px, s=s, cur=cur: e.matmul(px[:, s * 128:(s + 1) * 128], Yn[cur][:, s, :], Xn[cur][:, s, :], start=True, stop=True),
                             (("Yn", cur), ("Xn", cur)), (pxk,))
                    if p < 5:
                        py, pyk = nf()
                        for s in range(4):
                            P.op("tensor", lambda e, py=py, s=s, cur=cur: e.matmul(py[:, s * 128:(s + 1) * 128], Xn[cur][:, s, :], Yn[cur][:, s, :], start=True, stop=True),
                                 (("Yn", cur), ("Xn", cur)), (pyk,))
                    evac(Xn[nxt][:].rearrange("p a b -> p (a b)"), px[:], (pxk,), (("Xn", nxt),))
                    if p < 5:
                        evac(Yn[nxt][:].rearrange("p a b -> p (a b)"), py[:], (pyk,), (("Yn", nxt),))
                    pr, prk = nf()
                    for s in range(4):
                        P.op("tensor", lambda e, pr=pr, s=s, cur=cur, nxt=nxt: e.matmul(pr[:, s * 128:(s + 1) * 128], Xn[nxt][:, s, :], Rn[cur][:, s, :], start=True, stop=False),
                             (("Xn", nxt), ("Rn", cur)), (prk,))
                        P.op("tensor", lambda e, pr=pr, s=s, cur=cur: e.matmul(pr[:, s * 128:(s + 1) * 128], identb[:], Rn[cur][:, s, :], start=False, stop=True),
                             ("identb", ("Rn", cur)), (prk,))
                    evac(Rn[nxt][:].rearrange("p a b -> p (a b)"), pr[:], (prk,), (("Rn", nxt),))
                    cur = nxt
                TT, TTk = Rn[cur], ("Rn", cur)
                pw, pwk = nf()
                for s in range(4):
                    P.op("tensor", lambda e, pw=pw, s=s, TT=TT: e.matmul(pw[:, s * 128:(s + 1) * 128], Kbg[:, s, :], TT[:, s, :], start=True, stop=True), ("Kbg", TTk), (pwk,))
                P.op("vector", lambda e, pw=pw: e.tensor_scalar(nWT[:].rearrange("p a b -> p (a b)"), pw[:], -1.0, None, ALU.mult), (pwk,), ("nWT",))
                dg, dgk = tf()
                P.op("vector", lambda e, dg=dg, h=h: e.tensor_tensor(dg[:].rearrange("p (a b) -> p a b", a=4), cst[:, 0, :].unsqueeze(1).broadcast_to([128, 4, 128]),
                                                              sc[:, 2, :, h:h + 1].broadcast_to([128, 4, 128]), ALU.mult), ("cst", "sce"), (dgk,))
                pe_, pek = nf()
                P.op("tensor", lambda e, pe_=pe_, dg=dg: e.matmul(pe_[:], cst[:, 1, :], dg[:], start=True, stop=True), ("cst", dgk), (pek,))
                P.op("vector", lambda e, pe_=pe_: e.scalar_tensor_tensor(qgT[:], pe_[:], 128.0 ** -0.5, qn[:], ALU.mult, ALU.mult), (pek, "qn"), ("qgT",))

                Sfk, Sbk = ("Sf", h), ("Sb", h)
                for n in range(8):
                    s, half = n // 2, n % 2
                    r0 = half * 64
                    c0 = s * 128 + r0
                    vk = ("pch_v", half)
                    P.op("tensor", lambda e, s=s, r0=r0, TT=TT: e.matmul(pchain[r0:r0 + 64, 0:128], TT[:, s, r0:r0 + 64], Vb[:, s, :], start=True, stop=False), (TTk, "Vb"), (vk,))
                    P.op("tensor", lambda e, s=s, r0=r0, h=h: e.matmul(pchain[r0:r0 + 64, 0:128], nWT[:, s, r0:r0 + 64], Sb[h][:], start=False, stop=True), ("nWT", Sbk), (vk,))
                    vnk = ("vnew",)
                    P.op("scalar", lambda e, r0=r0: e.copy(vnew[r0:r0 + 64, :], pchain[r0:r0 + 64, 0:128]), (vk,), (vnk,))
                    ok_ = ("pch_o", s)
                    oreg = pchain[r0:r0 + 64, 0:128]
                    P.op("tensor", lambda e, s=s, r0=r0, c0=c0, h=h, n=n: e.matmul(pf_o[r0:r0 + 64, s * 128:(s + 1) * 128], qgT[:, c0:c0 + 64], Sb[h][:], start=True, stop=False), ("qgT", Sbk), (("pfo",),))
                    P.op("tensor", lambda e, s=s, r0=r0: e.matmul(pf_o[r0:r0 + 64, s * 128:(s + 1) * 128], intraT[:, s, r0:r0 + 64], vnew[:, :], start=False, stop=True), ("intraT", vnk), (("pfo",),))
                    dk_ = ("pch_d",)
                    P.op("tensor", lambda e, s=s, half=half: e.matmul(pchd[:, 0:128], (Kg0, Kg1)[half][:, s, :], vnew[:, :], start=True, stop=True), ("Kg0", "Kg1", vnk), (dk_,))
                    P.op("vector", lambda e, s=s, h=h, half=half: e.scalar_tensor_tensor(Sb[h][:], Sf[h][:], sc[:, 4 + half, s, h:h + 1], pchd[:, 0:128], ALU.mult, ALU.add),
                         (Sfk, "sce", dk_), (Sbk,))
                    P.op("vector", lambda e, s=s, h=h, half=half: e.scalar_tensor_tensor(Sf[h][:], Sf[h][:], sc[:, 4 + half, s, h:h + 1], pchd[:, 0:128], ALU.mult, ALU.add),
                         (Sfk, "sce", dk_), (Sfk,))
                P.op("scalar", lambda e: e.copy(osb[:].rearrange("p a b -> p (a b)"), pf_o[:]), (("pfo",),), ("osb",))
                sq, sqk = tf()
                P.op("gpsimd", lambda e, sq=sq: e.tensor_tensor(sq[:], osb[:].rearrange("p a b -> p (a b)"), osb[:].rearrange("p a b -> p (a b)"), ALU.mult), ("osb",), (sqk,))
                P.op("vector", lambda e, sq=sq: e.tensor_reduce(st4[:, 4:8], sq[:].rearrange("p (a b) -> p a b", a=4), AX.X, ALU.add), (sqk,), ("st4o",))
                rstd_small(st4[:, 4:8], st4[:, 4:8], 1.0 / 128, ("st4o",), ("st4o",))
                P.op("vector", lambda e: e.tensor_tensor(osb[:], osb[:], st4[:, 4:8].unsqueeze(2).broadcast_to([128, 4, 128]), ALU.mult), ("osb", "st4o"), ("osb",))
                P.op("gpsimd", lambda e: e.tensor_tensor(osb[:], osb[:], pv[:, PV_NW:PV_NW + 128].unsqueeze(1).broadcast_to([128, 4, 128]), ALU.mult), ("osb", "pv"), ("osb",))
                P.op("vector", lambda e, h=h: e.tensor_tensor(omix[:], osb[:], gate[:, :, h * 128:(h + 1) * 128], ALU.mult), ("osb",) + tuple(("gate", s) for s in range(4)), ("omix",))
                ptile, pk = nt()
                for s in range(4):
                    P.op("tensor", lambda e, ptile=ptile, s=s: e.transpose(ptile[:, s * 128:(s + 1) * 128], omix[:, s, :], identb[:]), ("omix", "identb"), (pk,))
                oi = rot["outb"] = (rot["outb"] + 1) % 2
                ok = ("outb", oi)
                evac(outb[oi][:], ptile[:, 0:512], (pk,), (ok,))
                P.dma("sync", lambda e, oi=oi, h=h, t0=t0: e.dma_start(out=mixT[256 + h * 128:256 + (h + 1) * 128, t0:t0 + T], in_=outb[oi][:]), ok, (ok,), ())
        P.emit()
    return nc


def build_phase_b(ntok):
    T = 512
    ntiles = ntok // T
    nc = bass.Bass("TRN2", target_bir_lowering=False)
    x = nc.dram_tensor("x", [ntok, D], F32, kind="ExternalInput").ap()
    mixT = nc.dram_tensor("mixT", [D, ntok], BF16, kind="ExternalInput").ap()
    wout = nc.dram_tensor("wout", [D, D], F32, kind="ExternalInput").ap()
    wup = nc.dram_tensor("wup", [D, DFF], F32, kind="ExternalInput").ap()
    wdown = nc.dram_tensor("wdown", [DFF, D], F32, kind="ExternalInput").ap()
    n2_d = nc.dram_tensor("n2", [128, 16], F32, kind="ExternalInput").ap()
    fnw_d = nc.dram_tensor("fnw", [128, D], F32, kind="ExternalInput").ap()
    ident_d = nc.dram_tensor("ident", [128, 128], F32, kind="ExternalInput").ap()
    out = nc.dram_tensor("out", [ntok, D], F32, kind="ExternalOutput").ap()

    es = ExitStack()
    with es:
        P = Prog(nc, es)

        def sb(name, shape, dt):
            return es.enter_context(nc.sbuf_tensor("sb_" + name, shape, dt))

        def ps(name, shape, dt):
            return es.enter_context(nc.psum_tensor("ps_" + name, shape, dt))

        identf = sb("identf", [128, 128], F32)
        identb = sb("identb", [128, 128], BF16)
        n2 = sb("n2", [128, 16], F32)
        fnw = sb("fnw", [128, D], F32)
        h1 = [sb("h1_%d" % i, [128, D], F32) for i in range(4)]
        mx = sb("mx", [128, 16, T], BF16)
        hT = mx
        stg = [sb("stg%d" % i, [128, 4096], F32) for i in range(2)]
        hb = [sb("hb%d" % i, [128, D], BF16) for i in range(2)]
        junk = sb("junk", [128, D], BF16)
        st4 = sb("st4", [128, 8], F32)
        wu = [sb("wu%d" % i, [128, 16, 512], BF16) for i in range(2)]
        wd = [sb("wd%d" % i, [128, 4, D], BF16) for i in range(2)]
        aT = [sb("aT%d" % i, [128, 4, T], BF16) for i in range(2)]
        rl = [sb("rl%d" % i, [128, T], F32) for i in range(2)]
        pf = [ps("pf%d" % i, [128, T], F32) for i in range(6)]
        pt = [ps("pt%d" % i, [128, 1024], BF16) for i in range(2)]
        rot = {"f": 0, "t": 0, "wu": 0, "wd": 0, "rl": 0, "hb": 0, "stg": 0}

        def nf():
            rot["f"] = (rot["f"] + 1) % 6
            return pf[rot["f"]], ("pf", rot["f"])

        def nt():
            rot["t"] = (rot["t"] + 1) % 2
            return pt[rot["t"]], ("pt", rot["t"])

        P.dma("sync", lambda e: e.dma_start(out=identf[:], in_=ident_d[:, :]), "identf", (), ("identf",))
        P.dma("sync", lambda e: e.dma_start(out=n2[:], in_=n2_d[:, :]), "n2", (), ("n2",))
        P.dma("sync", lambda e: e.dma_start(out=fnw[:], in_=fnw_d[:, :]), "fnw", (), ("fnw",))
        P.op("vector", lambda e: e.tensor_copy(identb[:], identf[:]), ("identf",), ("identb",))

        cast_rr = [0]

        def load_cast(dst_tile, dkey, src_ap, nhalf_dim):
            hsz = nhalf_dim // 2
            for half in range(2):
                j = rot["stg"] = (rot["stg"] + 1) % 2
                sk = ("stg", j)
                src_h = src_ap[:, half * hsz:(half + 1) * hsz, :]
                inner = src_h.shape[2]
                sv = stg[j][:].rearrange("p (a b) -> p a b", b=inner)
                P.dma("sync", lambda e, sv=sv, src_h=src_h: e.dma_start(out=sv, in_=src_h), sk, (), (sk,))
                dv = dst_tile[:, half * hsz:(half + 1) * hsz, :]
                cast_rr[0] ^= 1
                if cast_rr[0]:
                    P.op("scalar", lambda e, dv=dv, sv=sv: e.copy(dv, sv), (sk,), (dkey,))
                else:
                    P.op("gpsimd", lambda e, dv=dv, sv=sv: e.tensor_copy(dv, sv), (sk,), (dkey,))

        def load_wu(src_ap):
            i = rot["wu"] = (rot["wu"] + 1) % 2
            k = ("wu", i)
            load_cast(wu[i], k, src_ap, 16)
            return wu[i], k

        def load_wd(src_ap):
            i = rot["wd"] = (rot["wd"] + 1) % 2
            k = ("wd", i)
            load_cast(wd[i], k, src_ap, 4)
            return wd[i], k

        def rstd_small(dst_ap, scale, key):
            P.op("vector", lambda e: e.tensor_scalar(dst_ap, dst_ap, scale, EPS, ALU.mult, ALU.add), key, key)
            P.op("scalar", lambda e: e.activation(dst_ap, dst_ap, AF.Ln), key, key)
            P.op("scalar", lambda e: e.activation(dst_ap, dst_ap, AF.Exp, scale=-0.5), key, key)

        H1 = tuple(("h1", s) for s in range(4))
        for ti in range(ntiles):
            t0 = ti * T
            P.dma("sync", lambda e, t0=t0: e.dma_start(out=mx[:], in_=mixT[:, t0:t0 + T].rearrange("(c p) t -> p c t", p=128)), "mx", (), ("mx",))
            for s in range(4):
                P.dma("sync", lambda e, s=s, r0=t0 + s * 128: e.dma_start(out=h1[s][:], in_=x[r0:r0 + 128, :]), ("h1", s), (), (("h1", s),))
            for nb in range(4):
                wt, wk = load_wu(wout[:, nb * 512:(nb + 1) * 512].rearrange("(c p) n -> p c n", p=128))
                for s in range(4):
                    pp, ppk = nf()
                    for c in range(16):
                        P.op("tensor", lambda e, pp=pp, wt=wt, s=s, c=c: e.matmul(pp[:], mx[:, c, s * 128:(s + 1) * 128], wt[:, c, :], start=(c == 0), stop=(c == 15)),
                             ("mx", wk), (ppk,))
                    P.op("vector", lambda e, pp=pp, s=s, nb=nb: e.tensor_tensor(h1[s][:, nb * 512:(nb + 1) * 512], pp[:], h1[s][:, nb * 512:(nb + 1) * 512], ALU.add),
                         (ppk, ("h1", s)), (("h1", s),))
            for s in range(4):
                P.op("scalar", lambda e, s=s: e.activation(junk[:], h1[s][:], AF.Square, accum_out=st4[:, s:s + 1]), (("h1", s),), ("junk", ("st4", s)))
                rstd_small(st4[:, s:s + 1], 1.0 / D, (("st4", s),))
                hi = rot["hb"] = (rot["hb"] + 1) % 2
                hk = ("hb", hi)
                P.op("gpsimd", lambda e, s=s, hi=hi: e.tensor_scalar(hb[hi][:], h1[s][:], st4[:, s:s + 1], None, ALU.mult), (("h1", s), ("st4", s)), (hk,))
                for c4 in range(4):
                    ptile, pk = nt()
                    for cc in range(4):
                        c = c4 * 4 + cc
                        P.op("tensor", lambda e, ptile=ptile, cc=cc, c=c, hi=hi: e.transpose(ptile[:, cc * 128:(cc + 1) * 128], hb[hi][:, c * 128:(c + 1) * 128], identb[:]),
                             (hk, "identb"), (pk,))
                    P.op("vector", lambda e, ptile=ptile, c4=c4, s=s: e.tensor_tensor(
                        hT[:, c4 * 4:c4 * 4 + 4, s * 128:(s + 1) * 128],
                        ptile[:, 0:512].rearrange("p (a b) -> p a b", a=4),
                        n2[:, c4 * 4:c4 * 4 + 4].unsqueeze(2).broadcast_to([128, 4, 128]), ALU.mult),
                        (pk, "n2"), ("mx",))
            for jb in range(16):
                wt, wk = load_wu(wup[:, jb * 512:(jb + 1) * 512].rearrange("(c p) n -> p c n", p=128))
                dt_, dk = load_wd(wdown[jb * 512:(jb + 1) * 512, :].rearrange("(c p) n -> p c n", p=128))
                ai = jb % 2
                ak = ("aT", ai)
                for jc in range(4):
                    pp, ppk = nf()
                    for c in range(16):
                        P.op("tensor", lambda e, pp=pp, wt=wt, jc=jc, c=c: e.matmul(pp[:], wt[:, c, jc * 128:(jc + 1) * 128], hT[:, c, :], start=(c == 0), stop=(c == 15)),
                             ("mx", wk), (ppk,))
                    ri = rot["rl"] = (rot["rl"] + 1) % 2
                    rk = ("rl", ri)
                    P.op("scalar", lambda e, pp=pp, ri=ri: e.activation(rl[ri][:], pp[:], AF.Relu), (ppk,), (rk,))
                    P.op("gpsimd", lambda e, ri=ri, ai=ai, jc=jc: e.tensor_tensor(aT[ai][:, jc, :], rl[ri][:], rl[ri][:], ALU.mult), (rk,), (ak,))
                for s in range(4):
                    for nb in range(4):
                        pp, ppk = nf()
                        for jc in range(4):
                            P.op("tensor", lambda e, pp=pp, dt_=dt_, jc=jc, s=s, nb=nb, ai=ai: e.matmul(pp[:], aT[ai][:, jc, s * 128:(s + 1) * 128], dt_[:, jc, nb * 512:(nb + 1) * 512], start=(jc == 0), stop=(jc == 3)),
                                 (ak, dk), (ppk,))
                        P.op("vector", lambda e, pp=pp, s=s, nb=nb: e.tensor_tensor(h1[s][:, nb * 512:(nb + 1) * 512], pp[:], h1[s][:, nb * 512:(nb + 1) * 512], ALU.add),
                             (ppk, ("h1", s)), (("h1", s),))
            for s in range(4):
                P.op("scalar", lambda e, s=s: e.activation(junk[:], h1[s][:], AF.Square, accum_out=st4[:, 4 + s:5 + s]), (("h1", s),), ("junk", ("st4", 4 + s)))
                rstd_small(st4[:, 4 + s:5 + s], 1.0 / D, (("st4", 4 + s),))
                P.op("vector", lambda e, s=s: e.scalar_tensor_tensor(h1[s][:], h1[s][:], st4[:, 4 + s:5 + s], fnw[:], ALU.mult, ALU.mult), (("h1", s), ("st4", 4 + s), "fnw"), (("h1", s),))
                P.dma("sync", lambda e, s=s, r0=t0 + s * 128: e.dma_start(out=out[r0:r0 + 128, :], in_=h1[s][:]), ("h1o", s), (("h1", s),), ())
        P.emit()
    return nc


_CACHE = {}


def _get(name, fn, arg):
    k = (name, arg)
    if k not in _CACHE:
        _CACHE[k] = fn(arg)
    return _CACHE[k]


def run_phase_a(inp):
    x = np.asarray(inp["x"], np.float32)
    B, S, _ = x.shape
    w_in = np.asarray(inp["w_in"], np.float32)[0]
    b_glu = np.asarray(inp["b_glu"], np.float32)[0]
    dw_w = np.asarray(inp["conf_dw_w"], np.float32)[0]
    dw_b = np.asarray(inp["conf_dw_b"], np.float32)[0]
    ln_g = np.asarray(inp["conf_ln_g"], np.float32)[0]
    ln_b = np.asarray(inp["conf_ln_b"], np.float32)[0]
    dnw = np.asarray(inp["dn_conv_w"], np.float32)[0]
    a_log = np.asarray(inp["dn_a_log"], np.float32)[0]
    dtb = np.asarray(inp["dn_dt_bias"], np.float32)[0]
    nw = np.asarray(inp["dn_norm_w"], np.float32)[0]
    n1 = np.asarray(inp["norm1_w"], np.float32)[0]
    consts = _consts()
    in_maps = []
    for c in range(8):
        b, hp = c // 4, c % 4
        gs = (2 * hp, 2 * hp + 1)
        cols = []
        for off in (0, 1024, 2048, 3072, 4096, 5120):
            for g in gs:
                cols.extend(range(off + g * 128, off + (g + 1) * 128))
        cols.extend([6144 + g for g in gs])
        cols.extend([6152 + g for g in gs])
        win_c = np.ascontiguousarray(w_in[:, cols])
        pvec = np.zeros((128, PV_N), np.float32)
        pvec[:, PV_N1:PV_N1 + 16] = n1.reshape(16, 128).T
        for i, g in enumerate(gs):
            sl = slice(g * 128, (g + 1) * 128)
            pvec[:, PV_BV + i] = b_glu[sl]
            pvec[:, PV_BG + i] = b_glu[1024 + g * 128:1024 + (g + 1) * 128]
            pvec[:, PV_DWB + i] = dw_b[sl]
            pvec[:, PV_LNG + i] = ln_g[sl]
            pvec[:, PV_LNB + i] = ln_b[sl]
            pvec[:, PV_DWW + i * 31:PV_DWW + (i + 1) * 31] = dw_w[:, sl].T
            for j in range(3):
                pvec[:, PV_DNW + (i * 3 + j) * 4:PV_DNW + (i * 3 + j) * 4 + 4] = dnw[:, j * 1024 + g * 128:j * 1024 + (g + 1) * 128].T
            pvec[:, PV_ALOG + i] = a_log[g]
            pvec[:, PV_DTB + i] = dtb[g]
        pvec[:, PV_NW:PV_NW + 128] = nw[None, :]
        in_maps.append({"x": np.ascontiguousarray(x[b]), "win": win_c, "pvec": pvec, "consts": consts})
    nc = _get("a", build_phase_a, S)
    res = run_bass_kernel_spmd(nc, in_maps, core_ids=list(range(8)))
    mixT = np.zeros((B, D, S), ml_dtypes.bfloat16)
    for c in range(8):
        b, hp = c // 4, c % 4
        m = np.asarray(res.results[c]["mixT"])
        for i in range(2):
            g = 2 * hp + i
            mixT[b, g * 128:(g + 1) * 128] = m[i * 128:(i + 1) * 128]
            mixT[b, 1024 + g * 128:1024 + (g + 1) * 128] = m[256 + i * 128:256 + (i + 1) * 128]
    return mixT


def run_phase_b(inp, mixT):
    x = np.asarray(inp["x"], np.float32)
    B, S, _ = x.shape
    ntok = B * S // 8
    per_b = S // ntok
    n2 = np.ascontiguousarray(np.asarray(inp["norm2_w"], np.float32)[0].reshape(16, 128).T)
    fnw = np.ascontiguousarray(np.broadcast_to(np.asarray(inp["final_norm_w"], np.float32)[None, :], (128, D)))
    ident = np.eye(128, dtype=np.float32)
    wout = np.asarray(inp["w_out"], np.float32)[0]
    wup = np.asarray(inp["w_mlp_up"], np.float32)[0]
    wdown = np.asarray(inp["w_mlp_down"], np.float32)[0]
    in_maps = []
    for c in range(8):
        b, q = c // per_b, c % per_b
        sl = slice(q * ntok, (q + 1) * ntok)
        in_maps.append({"x": np.ascontiguousarray(x[b, sl]), "mixT": np.ascontiguousarray(mixT[b][:, sl]),
                        "wout": wout, "wup": wup, "wdown": wdown, "n2": n2, "fnw": fnw, "ident": ident})
    nc = _get("b", build_phase_b, ntok)
    res = run_bass_kernel_spmd(nc, in_maps, core_ids=list(range(8)))
    out = np.zeros((B, S, D), np.float32)
    for c in range(8):
        b, q = c // per_b, c % per_b
        out[b, q * ntok:(q + 1) * ntok] = np.asarray(res.results[c]["out"])
    return out


def kernel(**inputs):
    mixT = run_phase_a(inputs)
    return run_phase_b(inputs, mixT)
```
